# Optimizing a Trainium2 kernel written in Bass

```python
import jax, jax.numpy as jnp
from jax import lax
import numpy as np

D_MODEL = 1024
BATCH = 2
SEQ = 8192
DEPTH = 2

W_A = D_MODEL
K_A = 3
W_B = D_MODEL
K_B = 31
W_C = D_MODEL
POOL_WINDOWS = (2, 4, 8, 16)
N_POOL_GROUPS = len(POOL_WINDOWS)
GC = W_C // N_POOL_GROUPS
N_BRANCH = 3
D_FF = 4 * D_MODEL
EPS = 1e-6
COLS_A = 3 * W_A
COLS_B = 2 * W_B
COLS_C = W_C
COLS_G = N_BRANCH * D_MODEL
P_IN = COLS_A + COLS_B + COLS_C + COLS_G
SPLITS = (W_A, 2 * W_A, 3 * W_A, 3 * W_A + W_B, COLS_A + COLS_B, COLS_A + COLS_B + COLS_C)

kernel_name = "hybrid_conv_pool_gated_block"


def rmsnorm(x, g):
    x32 = x.astype(jnp.float32)
    y = x32 * lax.rsqrt(jnp.mean(x32 * x32, axis=-1, keepdims=True) + EPS)
    return (y * g.astype(jnp.float32)).astype(x.dtype)


def layernorm(x, g, b):
    x32 = x.astype(jnp.float32)
    mu = jnp.mean(x32, axis=-1, keepdims=True)
    xc = x32 - mu
    var = jnp.mean(xc * xc, axis=-1, keepdims=True)
    y = xc * lax.rsqrt(var + EPS) * g.astype(jnp.float32) + b.astype(jnp.float32)
    return y.astype(x.dtype)


def causal_depthwise_conv(u, w):
    k, c = w.shape
    return lax.conv_general_dilated(
        u, w[:, None, :].astype(u.dtype), window_strides=(1,), padding=[(k - 1, 0)],
        dimension_numbers=("NWC", "WIO", "NWC"), feature_group_count=c)


def short_conv_mixer(b_gate, c_gate, xh, conv_w, w_out):
    z = causal_depthwise_conv(c_gate * xh, conv_w)
    return (b_gate * z) @ w_out


def conformer_conv_mixer(val, gate, conv_w, conv_b, ln_g, ln_b, w_out, b_out):
    u = val * jax.nn.sigmoid(gate)
    u = causal_depthwise_conv(u, conv_w) + conv_b
    u = jax.nn.silu(layernorm(u, ln_g, ln_b))
    return u @ w_out + b_out


def pool_mixer(u, w_pool, scale):
    seq = u.shape[1]
    u32 = u.astype(jnp.float32)
    cs = jnp.cumsum(u32, axis=1)
    t = jnp.arange(seq)
    parts = []
    for g, w in enumerate(POOL_WINDOWS):
        sl = slice(g * GC, (g + 1) * GC)
        c = cs[..., sl]
        prev = jnp.pad(c, ((0, 0), (w, 0), (0, 0)))[:, :seq]
        cnt = jnp.minimum(t + 1, w).astype(jnp.float32)[None, :, None]
        parts.append((c - prev) / cnt - u32[..., sl])
    p = jnp.stack(parts, axis=2).astype(u.dtype)
    y = jnp.einsum('bsgc,gcd->bsgd', p, w_pool)
    return y.reshape(u.shape[0], seq, W_C) * scale


def setup_inputs(seed: int = 0) -> dict:
    key = jax.random.key(seed)
    ks = jax.random.split(key, 20)
    L, D = DEPTH, D_MODEL
    f32 = jnp.float32

    def nrm(k, shape, fan_in):
        return jax.random.normal(k, shape, f32) * (fan_in ** -0.5)

    def gain(k, shape):
        return 1.0 + 0.02 * jax.random.normal(k, shape, f32)

    return {
        "x": jax.random.normal(ks[0], (BATCH, SEQ, D), f32),
        "g_mix": gain(ks[1], (L, D)),
        "w_in": nrm(ks[2], (L, D, P_IN), D),
        "b_in": 0.02 * jax.random.normal(ks[3], (L, P_IN), f32),
        "conv_a": nrm(ks[4], (L, K_A, W_A), K_A),
        "w_out_a": nrm(ks[5], (L, W_A, D), W_A),
        "conv_b": nrm(ks[6], (L, K_B, W_B), K_B),
        "conv_b_bias": 0.02 * jax.random.normal(ks[7], (L, W_B), f32),
        "ln_b_g": gain(ks[8], (L, W_B)),
        "ln_b_b": 0.02 * jax.random.normal(ks[9], (L, W_B), f32),
        "w_out_b": nrm(ks[10], (L, W_B, D), W_B),
        "b_out_b": 0.02 * jax.random.normal(ks[11], (L, D), f32),
        "w_pool": nrm(ks[12], (L, N_POOL_GROUPS, GC, GC), GC),
        "pool_scale": gain(ks[13], (L, W_C)),
        "w_o": nrm(ks[14], (L, D, D), D),
        "g_mlp": gain(ks[15], (L, D)),
        "w_mlp1": nrm(ks[16], (L, D, D_FF), D),
        "w_mlp2": nrm(ks[17], (L, D_FF, D), D_FF),
        "g_final": gain(ks[18], (D,)),
    }


def reference(x, g_mix, w_in, b_in, conv_a, w_out_a, conv_b, conv_b_bias, ln_b_g, ln_b_b,
              w_out_b, b_out_b, w_pool, pool_scale, w_o, g_mlp, w_mlp1, w_mlp2, g_final):
    bsz, seq, d = x.shape
    for l in range(DEPTH):
        h = rmsnorm(x, g_mix[l])
        proj = h @ w_in[l] + b_in[l]
        a_b, a_c, a_x, b_val, b_gate, c_in, gates = jnp.split(proj, SPLITS, axis=-1)
        y_a = short_conv_mixer(a_b, a_c, a_x, conv_a[l], w_out_a[l])
        y_b = conformer_conv_mixer(b_val, b_gate, conv_b[l], conv_b_bias[l], ln_b_g[l],
                                   ln_b_b[l], w_out_b[l], b_out_b[l])
        y_c = pool_mixer(c_in, w_pool[l], pool_scale[l])
        g = jax.nn.sigmoid(gates).reshape(bsz, seq, N_BRANCH, d)
        merged = g[:, :, 0] * y_a + g[:, :, 1] * y_b + g[:, :, 2] * y_c
        x = x + merged @ w_o[l]
        h = rmsnorm(x, g_mlp[l])
        x = x + jnp.square(jax.nn.relu(h @ w_mlp1[l])) @ w_mlp2[l]
    return rmsnorm(x, g_final)
```

```python
import contextlib
import numpy as np
import concourse.bass as bass
import concourse.mybir as mybir
from concourse.bass_utils import run_bass_kernel_spmd

F32 = mybir.dt.float32
BF16 = mybir.dt.bfloat16
AF = mybir.ActivationFunctionType
ALU = mybir.AluOpType

D = 1024
NCH = 8
L = 2
SEQ = 8192
NCORE = 8
CHUNK_TOK = 2048
HALO = 64
NTOK = CHUNK_TOK + HALO
TILES = [(0, 448), (448, 416), (864, 416), (1280, 416), (1696, 416)]
TMAX = 448
MG = 32
NSLOT = 5
UC = 4096
EPS = 1e-6
POOLW = (2, 4, 8, 16)
KA, KB = 3, 31
ND = 8

O_GMIX, O_BIN, O_CA, O_CB, O_CBB, O_LNG, O_LNB, O_BOB, O_PS, O_GMLP = 0, 8, 80, 104, 352, 360, 368, 376, 384, 392
VPL = 400
O_GFIN = L * VPL
O_EPS = O_GFIN + 8
NV = O_EPS + 4
B_AB, B_AC, B_AX, B_BV, B_BG, B_CI, B_G0, B_G1, B_G2 = 0, 8, 16, 24, 32, 40, 48, 56, 64


def unit_list():
    u = []
    def two(nm, kind, base):
        u.append((nm + "0", UC, (kind, base)))
        u.append((nm + "1", UC, (kind, base + 512)))
    two("ax", "win", 2048)
    two("ac", "win", 1024)
    two("ab", "win", 0)
    u.append(("diag_a", 24 * 128, None))
    two("g0", "win", 6144)
    two("woa", "woa", 0)
    two("bg", "win", 4096)
    two("bv", "win", 3072)
    for c in range(NCH):
        u.append(("diag_b%d" % c, KB * 128, None))
    two("ci", "win", 5120)
    two("g2", "win", 8192)
    u.append(("pool", 2048, ("pool", 0)))
    two("g1", "win", 7168)
    two("wob", "wob", 0)
    two("wo", "wo", 0)
    for i in range(8):
        u.append(("m1_%d" % i, UC, ("mlp1", 512 * i)))
    for h in range(2):
        for kq in range(4):
            u.append(("m2_%d_%d" % (h, kq), UC, ("mlp2", (h, kq))))
    return u


UNITS = unit_list()
NU = len(UNITS)
SRC_IDX = {}
_si = 0
for _i, (_n, _c, _s) in enumerate(UNITS):
    if _s is not None:
        SRC_IDX[_i] = _si
        _si += 1
NSRC = _si


class Buf:
    __slots__ = ("w", "r", "dmaw", "dmar")

    def __init__(self):
        self.w = None
        self.r = {}
        self.dmaw = None
        self.dmar = None


class Prog:
    ENGS = ("pe", "act", "dve", "pool", "sp")

    def __init__(self):
        self.ops = {e: [] for e in self.ENGS}
        self.cnt = {e: 0 for e in self.ENGS}
        self.waited = {}
        self.dmacnt = {}

    def wait(self, eng, key, val):
        if val <= 0:
            return
        if self.waited.get((eng, key), 0) >= val:
            return
        self.waited[(eng, key)] = val
        self.ops[eng].append(("wait", key, val))

    def deps(self, eng, reads, writes):
        d = {}
        def upd(e, c):
            if c > d.get(e, 0):
                d[e] = c
        for b in reads:
            if b.w is not None:
                upd(*b.w)
            if b.dmaw is not None:
                self.wait(eng, b.dmaw[0], b.dmaw[1])
        for b in writes:
            if b.w is not None:
                upd(*b.w)
            for e, c in b.r.items():
                upd(e, c)
            if b.dmaw is not None:
                self.wait(eng, b.dmaw[0], b.dmaw[1])
            if b.dmar is not None:
                self.wait(eng, b.dmar[0], b.dmar[1])
        for e, c in d.items():
            if e == eng:
                if eng != "pe" and c == self.cnt[eng]:
                    self.wait(eng, e, c)
            else:
                self.wait(eng, e, c)

    def emit(self, eng, fn, inc=True, dmakey=None):
        if inc:
            self.cnt[eng] += 1
        self.ops[eng].append(("op", fn, inc, dmakey))
        if dmakey is not None:
            self.dmacnt[dmakey] = self.dmacnt.get(dmakey, 0) + 16
            return self.dmacnt[dmakey]
        return self.cnt[eng]

    def add(self, eng, fn, reads=(), writes=()):
        self.deps(eng, reads, writes)
        my = self.emit(eng, fn, True)
        for b in reads:
            b.r[eng] = my
        for b in writes:
            b.w = (eng, my)
            b.r = {}
            b.dmaw = None
            b.dmar = None
        return my


def build_program(debug=None):
    nc = bass.Bass("TRN2", target_bir_lowering=False)
    xT = nc.dram_tensor("xT", [128, NCH, NTOK], F32, kind="ExternalInput").ap()
    wsrc = nc.dram_tensor("wsrc", [L, NSRC, 128, UC], F32, kind="ExternalInput").ap()
    vecs_d = nc.dram_tensor("vecs", [128, NV], F32, kind="ExternalInput").ap()
    ident_d = nc.dram_tensor("ident", [128, 128], F32, kind="ExternalInput").ap()
    mask_d = nc.dram_tensor("mask", [128, HALO], F32, kind="ExternalInput").ap()
    icnt_d = nc.dram_tensor("icnt", [128, 4, 128], F32, kind="ExternalInput").ap()
    yT = nc.dram_tensor("yT", [128, NCH, CHUNK_TOK], F32, kind="ExternalOutput").ap()
    wbf = nc.dram_tensor("wbf", [L, NU, 128, UC], BF16, kind="Internal").ap()

    P = Prog()
    es = contextlib.ExitStack()
    with es:
        def sb(name, shape, dt):
            return es.enter_context(nc.sbuf_tensor(name, shape, dt))
        X = sb("X", [128, NCH, NTOK], F32)
        HN = sb("HN", [128, NCH, TMAX], BF16)
        R1 = sb("R1", [128, NCH, MG + TMAX], BF16)
        R2 = sb("R2", [128, NCH, TMAX], BF16)
        BIG = sb("BIG", [128, 16, MG + TMAX], F32)
        GT = sb("GT", [128, NCH, TMAX], BF16)
        SCR = sb("SCR", [128, 6, TMAX], F32)
        PP = sb("PP", [128, 2, MG + TMAX], F32)
        RING = sb("RING", [128, NSLOT, UC], BF16)
        VEC = sb("VEC", [128, NV], F32)
        IDN = sb("IDN", [128, 128], F32)
        ONES = sb("ONES", [128, 128], BF16)
        MASK = sb("MASK", [128, HALO], F32)
        ICNT = sb("ICNT", [128, 4, 128], F32)
        HA = sb("HA", [128, NCH, MG], BF16)
        HB = sb("HB", [128, NCH, MG], BF16)
        HC = sb("HC", [128, NCH, MG], F32)
        DGS = sb("DGS", [128, 2, UC], BF16)
        PS = es.enter_context(nc.psum_tensor("PS", [128, 8, 512], F32))

        bX = [[Buf() for _ in range(NCH)] for _ in TILES]
        bHN = [Buf() for _ in range(NCH)]
        bR1 = [Buf() for _ in range(NCH)]
        bR2 = [Buf() for _ in range(NCH)]
        bBIG = [Buf() for _ in range(16)]
        bGT = [Buf() for _ in range(NCH)]
        bSCR = [Buf() for _ in range(6)]
        bPP = [Buf() for _ in range(2)]
        bPS = [Buf() for _ in range(8)]
        bCONST = Buf()
        bHA = [Buf() for _ in range(NCH)]
        bHB = [Buf() for _ in range(NCH)]
        bHC = [Buf() for _ in range(NCH)]
        bDGS = [Buf(), Buf()]

        def vcol(i):
            return VEC[:, i:i + 1]

        def hid_ap(j, T):
            v = BIG[:, j // 2, :].bitcast(BF16)
            o = (j % 2) * TMAX
            return v[:, o:o + T]

        st = {"bank": 0, "q": 0}
        slot_last = [0] * NSLOT

        def next_bank():
            b = st["bank"]
            st["bank"] = (b + 1) % 8
            return b

        for (dst, src) in ((VEC[:, :], vecs_d), (IDN[:, :], ident_d), (MASK[:, :], mask_d), (ICNT[:, :, :], icnt_d)):
            P.emit("sp", (lambda e, d=dst, s=src: e.dma_start(out=d, in_=s)), inc=False, dmakey="cst")
        def xload(i):
            off, T = TILES[i]
            P.emit("sp", (lambda e, o=off, t=T: e.dma_start(out=X[:, :, o:o + t], in_=xT[:, :, o:o + t])),
                   inc=False, dmakey=("in", i))
        xload(0)
        xpending = list(range(1, len(TILES)))
        CST_ALL = 64

        P.wait("dve", "cst", CST_ALL)
        P.wait("act", "cst", CST_ALL)
        P.wait("pool", "cst", CST_ALL)
        P.add("dve", lambda e: e.memset(ONES[:, :], 1.0), writes=[bCONST])
        NCV = 12
        cv_of = {}
        cvj = [0]

        def conv_task(l):
            def run():
                for ui in range(NU):
                    if ui in SRC_IDX:
                        key = ("cv", cvj[0] % NCV)
                        v = P.emit("pool", (lambda e, l=l, ui=ui, si=SRC_IDX[ui]: e.dma_start(out=wbf[l, ui], in_=wsrc[l, si])),
                                   inc=False, dmakey=key)
                        cv_of[(l, ui)] = (key, v)
                        cvj[0] += 1
            return run

        ndg = [0]

        def diag_task(l, ui):
            def run():
                name, ncols, src = UNITS[ui]
                k = ndg[0] % 2
                ndg[0] += 1
                nblk = ncols // 128
                for blk in range(nblk):
                    if name == "diag_a":
                        c, tap = blk // KA, blk % KA
                        col = l * VPL + O_CA + tap * 8 + c
                    else:
                        c = int(name[6:])
                        col = l * VPL + O_CB + blk * 8 + c
                    P.add("dve", (lambda e, k=k, blk=blk, col=col: e.tensor_scalar(
                        out=DGS[:, k, blk * 128:(blk + 1) * 128], in0=IDN[:, :], scalar1=vcol(col), scalar2=None,
                        op0=ALU.mult)), writes=[bDGS[k]])
                key = ("dg", k)
                P.deps("pool", [bDGS[k]], [])
                v = P.emit("pool", (lambda e, l=l, ui=ui, k=k, n=ncols: e.dma_start(out=wbf[l, ui, :, 0:n], in_=DGS[:, k, 0:n])),
                           inc=False, dmakey=key)
                bDGS[k].dmar = (key, v)
                cv_of[(l, ui)] = (key, v)
            return run

        def diag_tasks(l):
            return [diag_task(l, ui) for ui in range(NU) if UNITS[ui][2] is None]

        LA = 3
        conv_queue = [(l, ui) for l in range(L) for ui in range(NU) if ui in SRC_IDX]

        def conv_one():
            if not conv_queue:
                return
            l, ui = conv_queue.pop(0)
            key = ("cv", cvj[0] % NCV)
            v = P.emit("pool", (lambda e, l=l, ui=ui, si=SRC_IDX[ui]: e.dma_start(out=wbf[l, ui], in_=wsrc[l, si])),
                       inc=False, dmakey=key)
            cv_of[(l, ui)] = (key, v)
            cvj[0] += 1

        for _ in range(LA):
            conv_one()
        pending0 = diag_tasks(0)
        pending = []
        for l in range(1, L):
            pending += diag_tasks(l)
        wcalls = [0]

        def prep_point():
            if pending and (cur["l"], cur["ti"]) >= (0, 1):
                pending.pop(0)()

        cur = {"l": 0, "ti": 0, "ui": 0}

        def wnext(name):
            if pending0:
                pending0.pop(0)()
            l, ti, ui = cur["l"], cur["ti"], cur["ui"]
            assert UNITS[ui][0] == name, (UNITS[ui][0], name)
            ncols = UNITS[ui][1]
            cur["ui"] = ui + 1
            q = st["q"]
            st["q"] = q + 1
            slot = q % NSLOT
            if q >= NSLOT:
                P.wait("sp", "pe", slot_last[slot])
            if ti == 0:
                key, v = cv_of[(l, ui)]
                P.wait("sp", key, v)
            v = P.emit("sp", (lambda e, l=l, ui=ui, s=slot, n=ncols: e.dma_start(out=RING[:, s, 0:n], in_=wbf[l, ui, :, 0:n])),
                       inc=False, dmakey=("w", slot))
            P.wait("pe", ("w", slot), v)
            wcalls[0] += 1
            if xpending and wcalls[0] % 8 == 0:
                xload(xpending.pop(0))
            if conv_queue:
                nl = conv_queue[0][0]
                if (nl == l and ti == 0) or (nl > l and ti >= 1 and wcalls[0] % 4 == 0):
                    P.wait("pool", ("w", slot), v)
                    conv_one()
            return RING[:, slot, :], slot

        def mm(bank, T, lhsT, rhs, start, stop, inc=False):
            return P.emit("pe", (lambda e, b=bank, T=T, a=lhsT, r=rhs, s0=start, s1=stop:
                                 e.matmul(PS[:, b, 0:T], lhsT=a, rhs=r, start=s0, stop=s1)), inc=inc)

        def group(T, pairs, reads, slot=None):
            b = next_bank()
            P.deps("pe", reads, [bPS[b]])
            n = len(pairs)
            for i, (a, r) in enumerate(pairs):
                my = mm(b, T, a, r, i == 0, i == n - 1, inc=(i == n - 1))
            for bf in reads:
                bf.r["pe"] = my
            bPS[b].w = ("pe", my)
            bPS[b].r = {}
            if slot is not None:
                slot_last[slot] = my
            return b

        def act(out, in_, func, reads, writes, bias=None, scale=None):
            kw = {}
            if bias is not None:
                kw["bias"] = bias
            if scale is not None:
                kw["scale"] = scale
            return P.add("act", (lambda e: e.activation(out=out, in_=in_, func=func, **kw)), reads, writes)

        def stt(eng, out, in0, scalar, in1, op0, op1, reads, writes):
            return P.add(eng, (lambda e: e.scalar_tensor_tensor(out=out, in0=in0, scalar=scalar, in1=in1, op0=op0, op1=op1)),
                         reads, writes)

        def tt(eng, out, in0, in1, op, reads, writes):
            return P.add(eng, (lambda e: e.tensor_tensor(out=out, in0=in0, in1=in1, op=op)), reads, writes)

        def ts(eng, out, in0, s1, s2, op0, op1, reads, writes):
            if s2 is None:
                return P.add(eng, (lambda e: e.tensor_scalar(out=out, in0=in0, scalar1=s1, scalar2=None, op0=op0)), reads, writes)
            return P.add(eng, (lambda e: e.tensor_scalar(out=out, in0=in0, scalar1=s1, scalar2=s2, op0=op0, op1=op1)), reads, writes)

        def cp(eng, out, in_, reads, writes):
            return P.add(eng, (lambda e: e.tensor_copy(out=out, in_=in_)), reads, writes)

        dumps = {}

        def dump(name, l, ti, ap_fn, bufs, T):
            if debug is None or debug != (l, ti):
                return
            d = nc.dram_tensor("dbg_" + name, [128, NCH, T], F32 if ap_fn(0).dtype == F32 else BF16, kind="ExternalOutput").ap()
            dumps[name] = d
            P.deps("pool", bufs, [])
            for c in range(NCH):
                v = P.emit("pool", (lambda e, c=c, a=ap_fn(c): e.dma_start(out=d[:, c, :], in_=a)), inc=False, dmakey="dbg")
            for b_ in bufs:
                b_.dmar = ("dbg", v)
            P.wait("pool", "dbg", v)

        def norm_sq(ti, off, T):
            for c in range(NCH):
                act(R2[:, c, 0:T], X[:, c, off:off + T], AF.Square, [bX[ti][c]], [bR2[c]])

        def norm_rest(T):
            b = group(T, [(ONES[:, :], R2[:, c, 0:T]) for c in range(NCH)], [bR2[c] for c in range(NCH)] + [bCONST])
            act(SCR[:, 0, 0:T], PS[:, b, 0:T], AF.Sqrt, [bPS[b]], [bSCR[0]], bias=vcol(O_EPS), scale=1.0 / D)
            P.add("dve", (lambda e: e.reciprocal(out=SCR[:, 1, 0:T], in_=SCR[:, 0, 0:T])), [bSCR[0]], [bSCR[1]])

        def norm_stats(ti, off, T):
            norm_sq(ti, off, T)
            norm_rest(T)

        def norm_apply(ti, off, T, gbase, out_fn, out_bufs):
            for c in range(NCH):
                stt("dve", out_fn(c), X[:, c, off:off + T], vcol(gbase + c), SCR[:, 1, 0:T], ALU.mult, ALU.mult,
                    [bX[ti][c], bSCR[1]], [out_bufs[c]])

        def rmsnorm_to(ti, off, T, gbase, out_fn, out_bufs):
            norm_stats(ti, off, T)
            norm_apply(ti, off, T, gbase, out_fn, out_bufs)

        def proj_pairs(slot_ap, mi, T):
            return [(slot_ap[:, kc * 512 + mi * 128: kc * 512 + (mi + 1) * 128], HN[:, kc, 0:T]) for kc in range(NCH)]

        def lin_pairs(slot_ap, mi, src, T):
            return [(slot_ap[:, kc * 512 + mi * 128: kc * 512 + (mi + 1) * 128], src[:, kc, 0:T]) for kc in range(NCH)]

        def margin_in(dst_margin, hist, bdst, bhist, ti):
            if ti == 0:
                P.add("pool", (lambda e: e.memset(dst_margin, 0.0)), [], [bdst])
            else:
                cp("pool", dst_margin, hist, [bhist], [bdst])

        def margin_out(hist, tail, bhist, bsrc):
            cp("pool", hist, tail, [bsrc], [bhist])

        for ti in range(len(TILES)):
            for c in range(NCH):
                bX[ti][c].dmaw = (("in", ti), 16)
        seq = [(l, ti) for l in range(L) for ti in range(len(TILES))]

        def hn_fn(T):
            return lambda c: HN[:, c, 0:T]

        for idx, (l, ti) in enumerate(seq):
            vb = l * VPL
            off, T = TILES[ti]
            nxt = seq[idx + 1] if idx + 1 < len(seq) else None
            if True:
                cur["l"], cur["ti"], cur["ui"] = l, ti, 0
                if idx == 0:
                    rmsnorm_to(ti, off, T, vb + O_GMIX, hn_fn(T), bHN)
                hn_reads = list(bHN)
                dump('hn', l, ti, (lambda c: HN[:, c, 0:T]), bHN, T)

                for half in range(2):
                    w, s = wnext("ax%d" % half)
                    for mi in range(4):
                        m = half * 4 + mi
                        b = group(T, proj_pairs(w, mi, T), hn_reads, s)
                        act(BIG[:, m, MG:MG + T], PS[:, b, 0:T], AF.Identity, [bPS[b]], [bBIG[m]], bias=vcol(vb + O_BIN + B_AX + m))
                for half in range(2):
                    w, s = wnext("ac%d" % half)
                    for mi in range(4):
                        m = half * 4 + mi
                        b = group(T, proj_pairs(w, mi, T), hn_reads, s)
                        margin_in(R1[:, m, 0:MG], HA[:, m, :], bR1[m], bHA[m], ti)
                        stt("dve", R1[:, m, MG:MG + T], PS[:, b, 0:T], vcol(vb + O_BIN + B_AC + m), BIG[:, m, MG:MG + T],
                            ALU.add, ALU.mult, [bPS[b], bBIG[m]], [bR1[m]])
                        if ti == 0:
                            tt("pool", R1[:, m, MG:MG + HALO], R1[:, m, MG:MG + HALO], MASK[:, :], ALU.mult, [bR1[m]], [bR1[m]])
                        margin_out(HA[:, m, :], R1[:, m, T:T + MG], bHA[m], bR1[m])
                for half in range(2):
                    w, s = wnext("ab%d" % half)
                    for mi in range(4):
                        m = half * 4 + mi
                        b = group(T, proj_pairs(w, mi, T), hn_reads, s)
                        act(BIG[:, m, MG:MG + T], PS[:, b, 0:T], AF.Identity, [bPS[b]], [bBIG[m]], bias=vcol(vb + O_BIN + B_AB + m))
                dump('ca', l, ti, (lambda c: R1[:, c, MG:MG + T]), bR1, T)
                dump('ab', l, ti, (lambda c: BIG[:, c, MG:MG + T]), bBIG[0:8], T)
                w, s = wnext("diag_a")
                for c in range(NCH):
                    pairs = [(w[:, (c * KA + k) * 128:(c * KA + k + 1) * 128],
                              R1[:, c, MG - (KA - 1) + k: MG - (KA - 1) + k + T]) for k in range(KA)]
                    b = group(T, pairs, [bR1[c]], s)
                    tt("dve", R2[:, c, 0:T], PS[:, b, 0:T], BIG[:, c, MG:MG + T], ALU.mult, [bPS[b], bBIG[c]], [bR2[c]])
                dump('zb', l, ti, (lambda c: R2[:, c, 0:T]), bR2, T)
                for half in range(2):
                    w, s = wnext("g0%d" % half)
                    for mi in range(4):
                        m = half * 4 + mi
                        b = group(T, proj_pairs(w, mi, T), hn_reads, s)
                        act(GT[:, m, 0:T], PS[:, b, 0:T], AF.Sigmoid, [bPS[b]], [bGT[m]], bias=vcol(vb + O_BIN + B_G0 + m))
                for half in range(2):
                    w, s = wnext("woa%d" % half)
                    for mi in range(4):
                        m = half * 4 + mi
                        b = group(T, lin_pairs(w, mi, R2, T), list(bR2), s)
                        tt("dve", BIG[:, 8 + m, 0:T], PS[:, b, 0:T], GT[:, m, 0:T], ALU.mult, [bPS[b], bGT[m]], [bBIG[8 + m]])

                dump('mgA', l, ti, (lambda c: BIG[:, 8 + c, 0:T]), bBIG[8:16], T)
                prep_point()
                for half in range(2):
                    w, s = wnext("bg%d" % half)
                    for mi in range(4):
                        m = half * 4 + mi
                        b = group(T, proj_pairs(w, mi, T), hn_reads, s)
                        act(BIG[:, m, MG:MG + T], PS[:, b, 0:T], AF.Sigmoid, [bPS[b]], [bBIG[m]], bias=vcol(vb + O_BIN + B_BG + m))
                for half in range(2):
                    w, s = wnext("bv%d" % half)
                    for mi in range(4):
                        m = half * 4 + mi
                        b = group(T, proj_pairs(w, mi, T), hn_reads, s)
                        margin_in(R1[:, m, 0:MG], HB[:, m, :], bR1[m], bHB[m], ti)
                        stt("dve", R1[:, m, MG:MG + T], PS[:, b, 0:T], vcol(vb + O_BIN + B_BV + m), BIG[:, m, MG:MG + T],
                            ALU.add, ALU.mult, [bPS[b], bBIG[m]], [bR1[m]])
                        if ti == 0:
                            tt("pool", R1[:, m, MG:MG + HALO], R1[:, m, MG:MG + HALO], MASK[:, :], ALU.mult, [bR1[m]], [bR1[m]])
                        margin_out(HB[:, m, :], R1[:, m, T:T + MG], bHB[m], bR1[m])
                dump('u', l, ti, (lambda c: R1[:, c, MG:MG + T]), bR1, T)
                for c in range(NCH):
                    w, s = wnext("diag_b%d" % c)
                    pairs = [(w[:, k * 128:(k + 1) * 128], R1[:, c, MG - (KB - 1) + k: MG - (KB - 1) + k + T]) for k in range(ND, KB)]
                    b = group(T, pairs, [bR1[c]], s)
                    kacc = c % 2
                    for k in range(ND):
                        ush = R1[:, c, MG - (KB - 1) + k: MG - (KB - 1) + k + T]
                        wk = vcol(vb + O_CB + k * 8 + c)
                        if k == 0:
                            ts("dve", PP[:, kacc, 0:T], ush, wk, None, ALU.mult, None, [bR1[c]], [bPP[kacc]])
                        else:
                            stt("dve", PP[:, kacc, 0:T], ush, wk, PP[:, kacc, 0:T], ALU.mult, ALU.add, [bR1[c], bPP[kacc]], [bPP[kacc]])
                    stt("dve", BIG[:, c, MG:MG + T], PS[:, b, 0:T], vcol(vb + O_CBB + c), PP[:, kacc, 0:T], ALU.add, ALU.add,
                        [bPS[b], bPP[kacc]], [bBIG[c]])
                    act(GT[:, c, 0:T], BIG[:, c, MG:MG + T], AF.Square, [bBIG[c]], [bGT[c]])
                    cp("pool", R2[:, c, 0:T], BIG[:, c, MG:MG + T], [bBIG[c]], [bR2[c]])
                dump('v', l, ti, (lambda c: BIG[:, c, MG:MG + T]), bBIG[0:8], T)
                b1 = group(T, [(ONES[:, :], R2[:, c, 0:T]) for c in range(NCH)], list(bR2) + [bCONST])
                b2 = group(T, [(ONES[:, :], GT[:, c, 0:T]) for c in range(NCH)], list(bGT) + [bCONST])
                ts("dve", SCR[:, 2, 0:T], PS[:, b1, 0:T], 1.0 / D, None, ALU.mult, None, [bPS[b1]], [bSCR[2]])
                tt("dve", SCR[:, 4, 0:T], SCR[:, 2, 0:T], SCR[:, 2, 0:T], ALU.mult, [bSCR[2]], [bSCR[4]])
                stt("dve", SCR[:, 5, 0:T], PS[:, b2, 0:T], 1.0 / D, SCR[:, 4, 0:T], ALU.mult, ALU.subtract, [bPS[b2], bSCR[4]], [bSCR[5]])
                act(SCR[:, 0, 0:T], SCR[:, 5, 0:T], AF.Sqrt, [bSCR[5]], [bSCR[0]], bias=vcol(O_EPS), scale=1.0)
                P.add("dve", (lambda e, T=T: e.reciprocal(out=SCR[:, 3, 0:T], in_=SCR[:, 0, 0:T])), [bSCR[0]], [bSCR[3]])
                for c in range(NCH):
                    tt("pool", BIG[:, c, MG:MG + T], BIG[:, c, MG:MG + T], SCR[:, 2, 0:T], ALU.subtract, [bBIG[c], bSCR[2]], [bBIG[c]])
                    tt("dve", BIG[:, c, MG:MG + T], BIG[:, c, MG:MG + T], SCR[:, 3, 0:T], ALU.mult, [bBIG[c], bSCR[3]], [bBIG[c]])
                    act(R2[:, c, 0:T], BIG[:, c, MG:MG + T], AF.Silu, [bBIG[c]], [bR2[c]],
                        bias=vcol(vb + O_LNB + c), scale=vcol(vb + O_LNG + c))
                dump('sb', l, ti, (lambda c: R2[:, c, 0:T]), bR2, T)
                for half in range(2):
                    w, s = wnext("ci%d" % half)
                    for mi in range(4):
                        m = half * 4 + mi
                        b = group(T, proj_pairs(w, mi, T), hn_reads, s)
                        margin_in(BIG[:, m, 0:MG], HC[:, m, :], bBIG[m], bHC[m], ti)
                        act(BIG[:, m, MG:MG + T], PS[:, b, 0:T], AF.Identity, [bPS[b]], [bBIG[m]], bias=vcol(vb + O_BIN + B_CI + m))
                        if ti == 0:
                            tt("pool", BIG[:, m, MG:MG + HALO], BIG[:, m, MG:MG + HALO], MASK[:, :], ALU.mult, [bBIG[m]], [bBIG[m]])
                        margin_out(HC[:, m, :], BIG[:, m, T:T + MG], bHC[m], bBIG[m])
                for c in range(NCH):
                    g = c // 2
                    wdw = POOLW[g]
                    src_ap, src_b = BIG[:, c, :], bBIG[c]
                    lo = MG - 15
                    for jstep in range(g + 1):
                        sh = 1 << jstep
                        lo += sh
                        k = jstep % 2
                        tt("dve", PP[:, k, lo:MG + T], src_ap[:, lo:MG + T], src_ap[:, lo - sh:MG + T - sh], ALU.add,
                           [src_b], [bPP[k]])
                        src_ap, src_b = PP[:, k, :], bPP[k]
                    stt("dve", R1[:, c, 0:T], src_ap[:, MG:MG + T], 1.0 / wdw, BIG[:, c, MG:MG + T], ALU.mult, ALU.subtract,
                        [src_b, bBIG[c]], [bR1[c]])
                    if ti == 0:
                        tt("dve", src_ap[:, MG:MG + 128], src_ap[:, MG:MG + 128], ICNT[:, g, :], ALU.mult, [src_b], [src_b])
                        tt("dve", R1[:, c, 0:128], src_ap[:, MG:MG + 128], BIG[:, c, MG:MG + 128], ALU.subtract,
                           [src_b, bBIG[c]], [bR1[c]])
                dump('mgB', l, ti, (lambda c: BIG[:, 8 + c, 0:T]), bBIG[8:16], T)
                prep_point()
                dump('cin', l, ti, (lambda c: BIG[:, c, MG:MG + T]), bBIG[0:8], T)
                for half in range(2):
                    w, s = wnext("g2%d" % half)
                    for mi in range(4):
                        m = half * 4 + mi
                        b = group(T, proj_pairs(w, mi, T), hn_reads, s)
                        act(GT[:, m, 0:T], PS[:, b, 0:T], AF.Sigmoid, [bPS[b]], [bGT[m]], bias=vcol(vb + O_BIN + B_G2 + m))
                dump('pf', l, ti, (lambda c: R1[:, c, 0:T]), bR1, T)
                w, s = wnext("pool")
                for g in range(4):
                    for mo in range(2):
                        m = 2 * g + mo
                        pairs = [(w[:, (g * 2 + kc) * 256 + mo * 128:(g * 2 + kc) * 256 + (mo + 1) * 128], R1[:, 2 * g + kc, 0:T])
                                 for kc in range(2)]
                        b = group(T, pairs, [bR1[2 * g], bR1[2 * g + 1]], s)
                        k = 4 + (m % 2)
                        stt("dve", SCR[:, k, 0:T], PS[:, b, 0:T], vcol(vb + O_PS + m), GT[:, m, 0:T], ALU.mult, ALU.mult,
                            [bPS[b], bGT[m]], [bSCR[k]])
                        tt("pool", BIG[:, 8 + m, 0:T], BIG[:, 8 + m, 0:T], SCR[:, k, 0:T], ALU.add, [bBIG[8 + m], bSCR[k]], [bBIG[8 + m]])

                for half in range(2):
                    w, s = wnext("g1%d" % half)
                    for mi in range(4):
                        m = half * 4 + mi
                        b = group(T, proj_pairs(w, mi, T), hn_reads, s)
                        act(GT[:, m, 0:T], PS[:, b, 0:T], AF.Sigmoid, [bPS[b]], [bGT[m]], bias=vcol(vb + O_BIN + B_G1 + m))
                for half in range(2):
                    w, s = wnext("wob%d" % half)
                    for mi in range(4):
                        m = half * 4 + mi
                        b = group(T, lin_pairs(w, mi, R2, T), list(bR2), s)
                        k = 4 + (m % 2)
                        stt("dve", SCR[:, k, 0:T], PS[:, b, 0:T], vcol(vb + O_BOB + m), GT[:, m, 0:T], ALU.add, ALU.mult,
                            [bPS[b], bGT[m]], [bSCR[k]])
                        tt("dve", R1[:, m, 0:T], BIG[:, 8 + m, 0:T], SCR[:, k, 0:T], ALU.add, [bBIG[8 + m], bSCR[k]], [bR1[m]])

                dump('mgb', l, ti, (lambda c: R2[:, c, 0:T]), bR2, T)
                for half in range(2):
                    w, s = wnext("wo%d" % half)
                    for mi in range(4):
                        m = half * 4 + mi
                        b = group(T, lin_pairs(w, mi, R1, T), list(bR1), s)
                        tt("dve", X[:, m, off:off + T], X[:, m, off:off + T], PS[:, b, 0:T], ALU.add, [bPS[b], bX[ti][m]], [bX[ti][m]])
                        act(HN[:, m, 0:T], X[:, m, off:off + T], AF.Identity, [bX[ti][m]], [bHN[m]], scale=vcol(vb + O_GMLP + m))

                dump('x1', l, ti, (lambda c: X[:, c, off:off + T]), bX[ti], T)
                prep_point()
                if nxt is not None:
                    noff, nT = TILES[nxt[1]]
                for i in range(8):
                    if i == 1:
                        for c in range(NCH):
                            act(R2[:, c, 0:T], X[:, c, off:off + T], AF.Square, [bX[ti][c]], [bR2[c]])
                    if i == 5 and nxt is not None:
                        norm_sq(nxt[1], noff, nT)
                    if i == 3:
                        bss = group(T, [(ONES[:, :], R2[:, c, 0:T]) for c in range(NCH)], list(bR2) + [bCONST])
                        ts("dve", SCR[:, 2, 0:T], PS[:, bss, 0:T], 1.0 / D, EPS, ALU.mult, ALU.add, [bPS[bss]], [bSCR[2]])
                        P.add("dve", (lambda e, T=T: e.reciprocal(out=SCR[:, 3, 0:T], in_=SCR[:, 2, 0:T])), [bSCR[2]], [bSCR[3]])
                    w, s = wnext("m1_%d" % i)
                    for mi in range(4):
                        jh = i * 4 + mi
                        b = group(T, proj_pairs(w, mi, T), hn_reads, s)
                        act(hid_ap(jh, T), PS[:, b, 0:T], AF.Relu, [bPS[b]], [bBIG[jh // 2]])
                        tt("dve", hid_ap(jh, T), hid_ap(jh, T), hid_ap(jh, T), ALU.mult, [bBIG[jh // 2]], [bBIG[jh // 2]])
                if nxt is not None:
                    norm_rest(nT)
                for h in range(2):
                    banks = [next_bank() for _ in range(4)]
                    P.deps("pe", [], [bPS[b] for b in banks])
                    for kq in range(4):
                        P.deps("pe", bBIG[kq * 4:(kq + 1) * 4], [])
                        w, s = wnext("m2_%d_%d" % (h, kq))
                        for mi in range(4):
                            for kc in range(NCH):
                                last = (kq == 3 and kc == NCH - 1)
                                endu = (mi == 3 and kc == NCH - 1)
                                my = mm(banks[mi], T, w[:, kc * 512 + mi * 128: kc * 512 + (mi + 1) * 128],
                                        hid_ap(kq * 8 + kc, T), kq == 0 and kc == 0, last, inc=(last or endu))
                                if last:
                                    bPS[banks[mi]].w = ("pe", my)
                                    bPS[banks[mi]].r = {}
                        slot_last[s] = my
                    for bf in bBIG:
                        bf.r["pe"] = my
                    if h == 0 and nxt is not None:
                        norm_apply(nxt[1], noff, nT, nxt[0] * VPL + O_GMIX, hn_fn(nT), bHN)
                    for mi in range(4):
                        m = h * 4 + mi
                        b = banks[mi]
                        k = 4 + (m % 2)
                        tt("dve", SCR[:, k, 0:T], PS[:, b, 0:T], SCR[:, 3, 0:T], ALU.mult, [bPS[b], bSCR[3]], [bSCR[k]])
                        tt("dve", X[:, m, off:off + T], X[:, m, off:off + T], SCR[:, k, 0:T], ALU.add, [bSCR[k], bX[ti][m]], [bX[ti][m]])
                dump('x2', l, ti, (lambda c: X[:, c, off:off + T]), bX[ti], T)
                assert cur["ui"] == NU
                prep_point()

                if l == L - 1:
                    rmsnorm_to(ti, off, T, O_GFIN, (lambda c, T=T: BIG[:, 8 + c, MG:MG + T]), bBIG[8:16])
                    lo = HALO if ti == 0 else 0
                    o0 = off + lo - HALO
                    n = T - lo
                    P.deps("pool", [bBIG[8 + c] for c in range(NCH)], [])
                    key = ("out", ti)
                    v = P.emit("pool", (lambda e, lo=lo, o0=o0, n=n: e.dma_start(out=yT[:, :, o0:o0 + n],
                                                                                   in_=BIG[:, NCH:2 * NCH, MG + lo:MG + lo + n])),
                               inc=False, dmakey=key)
                    for c in range(NCH):
                        bBIG[8 + c].dmar = (key, v)
        assert not pending and not pending0 and not xpending and not conv_queue
        for ti in range(len(TILES)):
            P.wait("pool", ("out", ti), 16)

        keys = set()
        for e in Prog.ENGS:
            for it in P.ops[e]:
                if it[0] == "wait":
                    keys.add(it[1])
                elif it[3] is not None:
                    keys.add(it[3])
        for e in ("pe", "act", "dve", "pool"):
            keys.add(e)
        sems = {}
        for k in sorted(keys, key=str):
            nm = "s_" + str(k).replace("(", "").replace(")", "").replace(",", "_").replace("'", "").replace(" ", "")
            sems[k] = es.enter_context(nc.semaphore(nm))

        def replay(name, eng):
            for it in P.ops[name]:
                if it[0] == "wait":
                    eng.wait_ge(sems[it[1]], it[2])
                else:
                    ins = it[1](eng)
                    if it[3] is not None:
                        ins.then_inc(sems[it[3]], 16)
                    elif it[2]:
                        ins.then_inc(sems[name], 1)

        with nc.Block() as block:
            @block.tensor
            def _(e):
                replay("pe", e)

            @block.scalar
            def _(e):
                replay("act", e)

            @block.vector
            def _(e):
                replay("dve", e)

            @block.gpsimd
            def _(e):
                replay("pool", e)

            @block.sync
            def _(e):
                replay("sp", e)
    return nc


def _chunkvec(v):
    return np.ascontiguousarray(v.reshape(-1, 128).T)


def _unit_from(wmat, colstart):
    return wmat.reshape(NCH, 128, -1)[:, :, colstart:colstart + 512].transpose(1, 0, 2).reshape(128, UC)


def prepare_inputs(x, g_mix, w_in, b_in, conv_a, w_out_a, conv_b, conv_b_bias, ln_b_g, ln_b_b,
                   w_out_b, b_out_b, w_pool, pool_scale, w_o, g_mlp, w_mlp1, w_mlp2, g_final):
    f = np.float32
    wsrc = np.zeros((L, NSRC, 128, UC), dtype=f)
    for l in range(L):
        mats = {"win": w_in[l], "woa": w_out_a[l], "wob": w_out_b[l], "wo": w_o[l], "mlp1": w_mlp1[l]}
        for ui, (name, ncols, src) in enumerate(UNITS):
            if src is None:
                continue
            si = SRC_IDX[ui]
            kind, arg = src
            if kind in mats:
                wsrc[l, si] = _unit_from(np.asarray(mats[kind], dtype=f), arg)
            elif kind == "mlp2":
                h, kq = arg
                wsrc[l, si] = np.asarray(w_mlp2[l], dtype=f).reshape(32, 128, D)[kq * 8:(kq + 1) * 8, :, h * 512:(h + 1) * 512] \
                    .transpose(1, 0, 2).reshape(128, UC)
            elif kind == "pool":
                wsrc[l, si, :, 0:2048] = np.asarray(w_pool[l], dtype=f).reshape(4, 2, 128, 256).transpose(2, 0, 1, 3).reshape(128, 2048)
    vecs = np.zeros((128, NV), dtype=f)
    for l in range(L):
        vb = l * VPL
        vecs[:, vb + O_GMIX:vb + O_GMIX + 8] = _chunkvec(g_mix[l])
        vecs[:, vb + O_BIN:vb + O_BIN + 72] = _chunkvec(b_in[l])
        for k in range(KA):
            vecs[:, vb + O_CA + k * 8:vb + O_CA + (k + 1) * 8] = _chunkvec(conv_a[l, k])
        for k in range(KB):
            vecs[:, vb + O_CB + k * 8:vb + O_CB + (k + 1) * 8] = _chunkvec(conv_b[l, k])
        vecs[:, vb + O_CBB:vb + O_CBB + 8] = _chunkvec(conv_b_bias[l])
        vecs[:, vb + O_LNG:vb + O_LNG + 8] = _chunkvec(ln_b_g[l])
        vecs[:, vb + O_LNB:vb + O_LNB + 8] = _chunkvec(ln_b_b[l])
        vecs[:, vb + O_BOB:vb + O_BOB + 8] = _chunkvec(b_out_b[l])
        vecs[:, vb + O_PS:vb + O_PS + 8] = _chunkvec(pool_scale[l])
        vecs[:, vb + O_GMLP:vb + O_GMLP + 8] = _chunkvec(g_mlp[l])
    vecs[:, O_GFIN:O_GFIN + 8] = _chunkvec(g_final)
    vecs[:, O_EPS:O_EPS + 4] = EPS
    ident = np.eye(128, dtype=f)
    in_maps = []
    for core in range(NCORE):
        bi, ch = core // 4, core % 4
        s0 = ch * CHUNK_TOK
        lo = s0 - HALO
        xt = np.zeros((NTOK, D), dtype=f)
        if lo >= 0:
            xt[:] = x[bi, lo:lo + NTOK]
        else:
            xt[-lo:] = x[bi, 0:NTOK + lo]
        xTl = np.ascontiguousarray(xt.T.reshape(NCH, 128, NTOK).transpose(1, 0, 2))
        pos = lo + np.arange(128)
        mask = np.broadcast_to((pos[:HALO] >= 0).astype(f)[None, :], (128, HALO)).copy()
        icnt = np.zeros((128, 4, 128), dtype=f)
        for g, wdw in enumerate(POOLW):
            cnt = np.where(pos >= 0, np.minimum(pos + 1, wdw), wdw).astype(f)
            icnt[:, g, :] = (1.0 / cnt)[None, :]
        in_maps.append({"xT": xTl, "wsrc": wsrc, "vecs": vecs, "ident": ident, "mask": mask, "icnt": icnt})
    return in_maps


_NC_CACHE = {}


def kernel(**inputs):
    inputs = {k: np.asarray(v) for k, v in inputs.items()}
    in_maps = prepare_inputs(**inputs)
    if "nc" not in _NC_CACHE:
        _NC_CACHE["nc"] = build_program()
    nc = _NC_CACHE["nc"]
    res = run_bass_kernel_spmd(nc, in_maps, core_ids=list(range(NCORE)))
    out = np.empty((2, SEQ, D), dtype=np.float32)
    for core in range(NCORE):
        bi, ch = core // 4, core % 4
        yT = np.asarray(res.results[core]["yT"])
        out[bi, ch * CHUNK_TOK:(ch + 1) * CHUNK_TOK, :] = yT.transpose(2, 1, 0).reshape(CHUNK_TOK, D)
    return out
```

```python
import contextlib
import numpy as np
import concourse.bass as bass
import concourse.mybir as mybir
from concourse.bass_utils import run_bass_kernel_spmd

F32 = mybir.dt.float32
BF16 = mybir.dt.bfloat16
AF = mybir.ActivationFunctionType
ALU = mybir.AluOpType

D = 1024
NCH = 8
L = 2
SEQ = 8192
NCORE = 8
CHUNK_TOK = 2048
HALO = 64
NTOK = CHUNK_TOK + HALO
TILES = [(0, 448), (448, 416), (864, 416), (1280, 416), (1696, 416)]
TMAX = 448
MG = 32
NSLOT = 5
UC = 4096
EPS = 1e-6
POOLW = (2, 4, 8, 16)
KA, KB = 3, 31
ND = 5

O_GMIX, O_BIN, O_CA, O_CB, O_CBB, O_LNG, O_LNB, O_BOB, O_PS, O_GMLP = 0, 8, 80, 104, 352, 360, 368, 376, 384, 392
VPL = 400
O_GFIN = L * VPL
O_EPS = O_GFIN + 8
NV = O_EPS + 4
B_AB, B_AC, B_AX, B_BV, B_BG, B_CI, B_G0, B_G1, B_G2 = 0, 8, 16, 24, 32, 40, 48, 56, 64


def unit_list():
    u = []
    def two(nm, kind, base):
        u.append((nm + "0", UC, (kind, base)))
        u.append((nm + "1", UC, (kind, base + 512)))
    two("ax", "win", 2048)
    two("ac", "win", 1024)
    two("ab", "win", 0)
    u.append(("diag_a", 24 * 128, None))
    two("g0", "win", 6144)
    two("woa", "woa", 0)
    two("bg", "win", 4096)
    two("bv", "win", 3072)
    for c in range(NCH):
        u.append(("diag_b%d" % c, KB * 128, None))
    two("ci", "win", 5120)
    two("g2", "win", 8192)
    u.append(("pool", 2048, ("pool", 0)))
    two("g1", "win", 7168)
    two("wob", "wob", 0)
    two("wo", "wo", 0)
    for i in range(8):
        u.append(("m1_%d" % i, UC, ("mlp1", 512 * i)))
    for h in range(2):
        for kq in range(4):
            u.append(("m2_%d_%d" % (h, kq), UC, ("mlp2", (h, kq))))
    return u


UNITS = unit_list()
NU = len(UNITS)
SRC_IDX = {}
_si = 0
for _i, (_n, _c, _s) in enumerate(UNITS):
    if _s is not None:
        SRC_IDX[_i] = _si
        _si += 1
NSRC = _si


class Buf:
    __slots__ = ("w", "r", "dmaw", "dmar")

    def __init__(self):
        self.w = None
        self.r = {}
        self.dmaw = None
        self.dmar = None


class Prog:
    ENGS = ("pe", "act", "dve", "pool", "sp")

    def __init__(self):
        self.ops = {e: [] for e in self.ENGS}
        self.cnt = {e: 0 for e in self.ENGS}
        self.waited = {}
        self.dmacnt = {}

    def wait(self, eng, key, val):
        if val <= 0:
            return
        if self.waited.get((eng, key), 0) >= val:
            return
        self.waited[(eng, key)] = val
        self.ops[eng].append(("wait", key, val))

    def deps(self, eng, reads, writes):
        d = {}
        def upd(e, c):
            if c > d.get(e, 0):
                d[e] = c
        for b in reads:
            if b.w is not None:
                upd(*b.w)
            if b.dmaw is not None:
                self.wait(eng, b.dmaw[0], b.dmaw[1])
        for b in writes:
            if b.w is not None:
                upd(*b.w)
            for e, c in b.r.items():
                upd(e, c)
            if b.dmaw is not None:
                self.wait(eng, b.dmaw[0], b.dmaw[1])
            if b.dmar is not None:
                self.wait(eng, b.dmar[0], b.dmar[1])
        for e, c in d.items():
            if e == eng:
                if eng != "pe" and c == self.cnt[eng]:
                    self.wait(eng, e, c)
            else:
                self.wait(eng, e, c)

    def emit(self, eng, fn, inc=True, dmakey=None):
        if inc:
            self.cnt[eng] += 1
        self.ops[eng].append(("op", fn, inc, dmakey))
        if dmakey is not None:
            self.dmacnt[dmakey] = self.dmacnt.get(dmakey, 0) + 16
            return self.dmacnt[dmakey]
        return self.cnt[eng]

    def add(self, eng, fn, reads=(), writes=()):
        self.deps(eng, reads, writes)
        my = self.emit(eng, fn, True)
        for b in reads:
            b.r[eng] = my
        for b in writes:
            b.w = (eng, my)
            b.r = {}
            b.dmaw = None
            b.dmar = None
        return my


def build_program(debug=None):
    nc = bass.Bass("TRN2", target_bir_lowering=False)
    xT = nc.dram_tensor("xT", [128, NCH, NTOK], F32, kind="ExternalInput").ap()
    wsrc = nc.dram_tensor("wsrc", [L, NSRC, 128, UC], F32, kind="ExternalInput").ap()
    vecs_d = nc.dram_tensor("vecs", [128, NV], F32, kind="ExternalInput").ap()
    ident_d = nc.dram_tensor("ident", [128, 128], F32, kind="ExternalInput").ap()
    mask_d = nc.dram_tensor("mask", [128, HALO], F32, kind="ExternalInput").ap()
    icnt_d = nc.dram_tensor("icnt", [128, 4, 128], F32, kind="ExternalInput").ap()
    yT = nc.dram_tensor("yT", [128, NCH, CHUNK_TOK], F32, kind="ExternalOutput").ap()
    wbf = nc.dram_tensor("wbf", [L, NU, 128, UC], BF16, kind="Internal").ap()

    P = Prog()
    es = contextlib.ExitStack()
    with es:
        def sb(name, shape, dt):
            return es.enter_context(nc.sbuf_tensor(name, shape, dt))
        X = sb("X", [128, NCH, NTOK], F32)
        HN = sb("HN", [128, NCH, TMAX], BF16)
        R1 = sb("R1", [128, NCH, MG + TMAX], BF16)
        R2 = sb("R2", [128, NCH, TMAX], BF16)
        BIG = sb("BIG", [128, 16, MG + TMAX], F32)
        GT = sb("GT", [128, NCH, TMAX], BF16)
        SCR = sb("SCR", [128, 6, TMAX], F32)
        PP = sb("PP", [128, 2, MG + TMAX], F32)
        RING = sb("RING", [128, NSLOT, UC], BF16)
        VEC = sb("VEC", [128, NV], F32)
        IDN = sb("IDN", [128, 128], F32)
        ONES = sb("ONES", [128, 128], BF16)
        MASK = sb("MASK", [128, HALO], F32)
        ICNT = sb("ICNT", [128, 4, 128], F32)
        HA = sb("HA", [128, NCH, MG], BF16)
        HB = sb("HB", [128, NCH, MG], BF16)
        HC = sb("HC", [128, NCH, MG], F32)
        DGS = sb("DGS", [128, 2, UC], BF16)
        PS = es.enter_context(nc.psum_tensor("PS", [128, 8, 512], F32))

        bX = [[Buf() for _ in range(NCH)] for _ in TILES]
        bHN = [Buf() for _ in range(NCH)]
        bR1 = [Buf() for _ in range(NCH)]
        bR2 = [Buf() for _ in range(NCH)]
        bBIG = [Buf() for _ in range(16)]
        bGT = [Buf() for _ in range(NCH)]
        bSCR = [Buf() for _ in range(6)]
        bPP = [Buf() for _ in range(2)]
        bPS = [Buf() for _ in range(8)]
        bCONST = Buf()
        bHA = [Buf() for _ in range(NCH)]
        bHB = [Buf() for _ in range(NCH)]
        bHC = [Buf() for _ in range(NCH)]
        bDGS = [Buf(), Buf()]

        def vcol(i):
            return VEC[:, i:i + 1]

        def hid_ap(j, T):
            v = BIG[:, j // 2, :].bitcast(BF16)
            o = (j % 2) * TMAX
            return v[:, o:o + T]

        st = {"bank": 0, "q": 0}
        slot_last = [0] * NSLOT

        def next_bank():
            b = st["bank"]
            st["bank"] = (b + 1) % 8
            return b

        for (dst, src) in ((VEC[:, :], vecs_d), (IDN[:, :], ident_d), (MASK[:, :], mask_d), (ICNT[:, :, :], icnt_d)):
            P.emit("sp", (lambda e, d=dst, s=src: e.dma_start(out=d, in_=s)), inc=False, dmakey="cst")
        def xload(i):
            off, T = TILES[i]
            P.emit("sp", (lambda e, o=off, t=T: e.dma_start(out=X[:, :, o:o + t], in_=xT[:, :, o:o + t])),
                   inc=False, dmakey=("in", i))
        xload(0)
        xpending = list(range(1, len(TILES)))
        CST_ALL = 64

        P.wait("dve", "cst", CST_ALL)
        P.wait("act", "cst", CST_ALL)
        P.wait("pool", "cst", CST_ALL)
        P.add("dve", lambda e: e.memset(ONES[:, :], 1.0), writes=[bCONST])
        NCV = 12
        cv_of = {}
        cvj = [0]

        def conv_task(l):
            def run():
                for ui in range(NU):
                    if ui in SRC_IDX:
                        key = ("cv", cvj[0] % NCV)
                        v = P.emit("pool", (lambda e, l=l, ui=ui, si=SRC_IDX[ui]: e.dma_start(out=wbf[l, ui], in_=wsrc[l, si])),
                                   inc=False, dmakey=key)
                        cv_of[(l, ui)] = (key, v)
                        cvj[0] += 1
            return run

        ndg = [0]

        def diag_task(l, ui):
            def run():
                name, ncols, src = UNITS[ui]
                k = ndg[0] % 2
                ndg[0] += 1
                nblk = ncols // 128
                for blk in range(nblk):
                    if name == "diag_a":
                        c, tap = blk // KA, blk % KA
                        col = l * VPL + O_CA + tap * 8 + c
                    else:
                        c = int(name[6:])
                        col = l * VPL + O_CB + blk * 8 + c
                    P.add("dve", (lambda e, k=k, blk=blk, col=col: e.tensor_scalar(
                        out=DGS[:, k, blk * 128:(blk + 1) * 128], in0=IDN[:, :], scalar1=vcol(col), scalar2=None,
                        op0=ALU.mult)), writes=[bDGS[k]])
                key = ("dg", k)
                P.deps("pool", [bDGS[k]], [])
                v = P.emit("pool", (lambda e, l=l, ui=ui, k=k, n=ncols: e.dma_start(out=wbf[l, ui, :, 0:n], in_=DGS[:, k, 0:n])),
                           inc=False, dmakey=key)
                bDGS[k].dmar = (key, v)
                cv_of[(l, ui)] = (key, v)
            return run

        def diag_tasks(l):
            return [diag_task(l, ui) for ui in range(NU) if UNITS[ui][2] is None]

        LA = 3
        conv_queue = [(l, ui) for l in range(L) for ui in range(NU) if ui in SRC_IDX]

        def conv_one():
            if not conv_queue:
                return
            l, ui = conv_queue.pop(0)
            key = ("cv", cvj[0] % NCV)
            v = P.emit("pool", (lambda e, l=l, ui=ui, si=SRC_IDX[ui]: e.dma_start(out=wbf[l, ui], in_=wsrc[l, si])),
                       inc=False, dmakey=key)
            cv_of[(l, ui)] = (key, v)
            cvj[0] += 1

        for _ in range(LA):
            conv_one()
        pending0 = diag_tasks(0)
        pending = []
        for l in range(1, L):
            pending += diag_tasks(l)
        wcalls = [0]

        def prep_point():
            if pending and (cur["l"], cur["ti"]) >= (0, 1):
                pending.pop(0)()

        cur = {"l": 0, "ti": 0, "ui": 0}

        def wnext(name):
            if pending0:
                pending0.pop(0)()
            l, ti, ui = cur["l"], cur["ti"], cur["ui"]
            assert UNITS[ui][0] == name, (UNITS[ui][0], name)
            ncols = UNITS[ui][1]
            cur["ui"] = ui + 1
            q = st["q"]
            st["q"] = q + 1
            slot = q % NSLOT
            if q >= NSLOT:
                P.wait("sp", "pe", slot_last[slot])
            if ti == 0:
                key, v = cv_of[(l, ui)]
                P.wait("sp", key, v)
            v = P.emit("sp", (lambda e, l=l, ui=ui, s=slot, n=ncols: e.dma_start(out=RING[:, s, 0:n], in_=wbf[l, ui, :, 0:n])),
                       inc=False, dmakey=("w", slot))
            P.wait("pe", ("w", slot), v)
            wcalls[0] += 1
            if xpending and wcalls[0] % 8 == 0:
                xload(xpending.pop(0))
            if conv_queue:
                nl = conv_queue[0][0]
                if (nl == l and ti == 0) or (nl > l and ti >= 1 and wcalls[0] % 4 == 0):
                    P.wait("pool", ("w", slot), v)
                    conv_one()
            return RING[:, slot, :], slot

        def mm(bank, T, lhsT, rhs, start, stop, inc=False):
            return P.emit("pe", (lambda e, b=bank, T=T, a=lhsT, r=rhs, s0=start, s1=stop:
                                 e.matmul(PS[:, b, 0:T], lhsT=a, rhs=r, start=s0, stop=s1)), inc=inc)

        def group(T, pairs, reads, slot=None):
            b = next_bank()
            P.deps("pe", reads, [bPS[b]])
            n = len(pairs)
            for i, (a, r) in enumerate(pairs):
                my = mm(b, T, a, r, i == 0, i == n - 1, inc=(i == n - 1))
            for bf in reads:
                bf.r["pe"] = my
            bPS[b].w = ("pe", my)
            bPS[b].r = {}
            if slot is not None:
                slot_last[slot] = my
            return b

        def act(out, in_, func, reads, writes, bias=None, scale=None):
            kw = {}
            if bias is not None:
                kw["bias"] = bias
            if scale is not None:
                kw["scale"] = scale
            return P.add("act", (lambda e: e.activation(out=out, in_=in_, func=func, **kw)), reads, writes)

        def stt(eng, out, in0, scalar, in1, op0, op1, reads, writes):
            return P.add(eng, (lambda e: e.scalar_tensor_tensor(out=out, in0=in0, scalar=scalar, in1=in1, op0=op0, op1=op1)),
                         reads, writes)

        def tt(eng, out, in0, in1, op, reads, writes):
            return P.add(eng, (lambda e: e.tensor_tensor(out=out, in0=in0, in1=in1, op=op)), reads, writes)

        def ts(eng, out, in0, s1, s2, op0, op1, reads, writes):
            if s2 is None:
                return P.add(eng, (lambda e: e.tensor_scalar(out=out, in0=in0, scalar1=s1, scalar2=None, op0=op0)), reads, writes)
            return P.add(eng, (lambda e: e.tensor_scalar(out=out, in0=in0, scalar1=s1, scalar2=s2, op0=op0, op1=op1)), reads, writes)

        def cp(eng, out, in_, reads, writes):
            return P.add(eng, (lambda e: e.tensor_copy(out=out, in_=in_)), reads, writes)

        dumps = {}

        def dump(name, l, ti, ap_fn, bufs, T):
            if debug is None or debug != (l, ti):
                return
            d = nc.dram_tensor("dbg_" + name, [128, NCH, T], F32 if ap_fn(0).dtype == F32 else BF16, kind="ExternalOutput").ap()
            dumps[name] = d
            P.deps("pool", bufs, [])
            for c in range(NCH):
                v = P.emit("pool", (lambda e, c=c, a=ap_fn(c): e.dma_start(out=d[:, c, :], in_=a)), inc=False, dmakey="dbg")
            for b_ in bufs:
                b_.dmar = ("dbg", v)
            P.wait("pool", "dbg", v)

        def norm_sq(ti, off, T):
            for c in range(NCH):
                act(R2[:, c, 0:T], X[:, c, off:off + T], AF.Square, [bX[ti][c]], [bR2[c]])

        def norm_rest(T):
            b = group(T, [(ONES[:, :], R2[:, c, 0:T]) for c in range(NCH)], [bR2[c] for c in range(NCH)] + [bCONST])
            act(SCR[:, 0, 0:T], PS[:, b, 0:T], AF.Sqrt, [bPS[b]], [bSCR[0]], bias=vcol(O_EPS), scale=1.0 / D)
            P.add("dve", (lambda e: e.reciprocal(out=SCR[:, 1, 0:T], in_=SCR[:, 0, 0:T])), [bSCR[0]], [bSCR[1]])

        def norm_stats(ti, off, T):
            norm_sq(ti, off, T)
            norm_rest(T)

        def norm_apply(ti, off, T, gbase, out_fn, out_bufs):
            for c in range(NCH):
                stt("dve", out_fn(c), X[:, c, off:off + T], vcol(gbase + c), SCR[:, 1, 0:T], ALU.mult, ALU.mult,
                    [bX[ti][c], bSCR[1]], [out_bufs[c]])

        def rmsnorm_to(ti, off, T, gbase, out_fn, out_bufs):
            norm_stats(ti, off, T)
            norm_apply(ti, off, T, gbase, out_fn, out_bufs)

        def proj_pairs(slot_ap, mi, T):
            return [(slot_ap[:, kc * 512 + mi * 128: kc * 512 + (mi + 1) * 128], HN[:, kc, 0:T]) for kc in range(NCH)]

        def lin_pairs(slot_ap, mi, src, T):
            return [(slot_ap[:, kc * 512 + mi * 128: kc * 512 + (mi + 1) * 128], src[:, kc, 0:T]) for kc in range(NCH)]

        def margin_in(dst_margin, hist, bdst, bhist, ti):
            if ti == 0:
                P.add("pool", (lambda e: e.memset(dst_margin, 0.0)), [], [bdst])
            else:
                cp("pool", dst_margin, hist, [bhist], [bdst])

        def margin_out(hist, tail, bhist, bsrc):
            cp("pool", hist, tail, [bsrc], [bhist])

        for ti in range(len(TILES)):
            for c in range(NCH):
                bX[ti][c].dmaw = (("in", ti), 16)
        seq = [(l, ti) for l in range(L) for ti in range(len(TILES))]

        def hn_fn(T):
            return lambda c: HN[:, c, 0:T]

        for idx, (l, ti) in enumerate(seq):
            vb = l * VPL
            off, T = TILES[ti]
            nxt = seq[idx + 1] if idx + 1 < len(seq) else None
            if True:
                cur["l"], cur["ti"], cur["ui"] = l, ti, 0
                if idx == 0:
                    rmsnorm_to(ti, off, T, vb + O_GMIX, hn_fn(T), bHN)
                hn_reads = list(bHN)
                dump('hn', l, ti, (lambda c: HN[:, c, 0:T]), bHN, T)

                for half in range(2):
                    w, s = wnext("ax%d" % half)
                    for mi in range(4):
                        m = half * 4 + mi
                        b = group(T, proj_pairs(w, mi, T), hn_reads, s)
                        act(BIG[:, m, MG:MG + T], PS[:, b, 0:T], AF.Identity, [bPS[b]], [bBIG[m]], bias=vcol(vb + O_BIN + B_AX + m))
                for half in range(2):
                    w, s = wnext("ac%d" % half)
                    for mi in range(4):
                        m = half * 4 + mi
                        b = group(T, proj_pairs(w, mi, T), hn_reads, s)
                        margin_in(R1[:, m, 0:MG], HA[:, m, :], bR1[m], bHA[m], ti)
                        stt("dve", R1[:, m, MG:MG + T], PS[:, b, 0:T], vcol(vb + O_BIN + B_AC + m), BIG[:, m, MG:MG + T],
                            ALU.add, ALU.mult, [bPS[b], bBIG[m]], [bR1[m]])
                        if ti == 0:
                            tt("pool", R1[:, m, MG:MG + HALO], R1[:, m, MG:MG + HALO], MASK[:, :], ALU.mult, [bR1[m]], [bR1[m]])
                        margin_out(HA[:, m, :], R1[:, m, T:T + MG], bHA[m], bR1[m])
                for half in range(2):
                    w, s = wnext("ab%d" % half)
                    for mi in range(4):
                        m = half * 4 + mi
                        b = group(T, proj_pairs(w, mi, T), hn_reads, s)
                        act(BIG[:, m, MG:MG + T], PS[:, b, 0:T], AF.Identity, [bPS[b]], [bBIG[m]], bias=vcol(vb + O_BIN + B_AB + m))
                dump('ca', l, ti, (lambda c: R1[:, c, MG:MG + T]), bR1, T)
                dump('ab', l, ti, (lambda c: BIG[:, c, MG:MG + T]), bBIG[0:8], T)
                w, s = wnext("diag_a")
                for c in range(NCH):
                    pairs = [(w[:, (c * KA + k) * 128:(c * KA + k + 1) * 128],
                              R1[:, c, MG - (KA - 1) + k: MG - (KA - 1) + k + T]) for k in range(KA)]
                    b = group(T, pairs, [bR1[c]], s)
                    tt("dve", R2[:, c, 0:T], PS[:, b, 0:T], BIG[:, c, MG:MG + T], ALU.mult, [bPS[b], bBIG[c]], [bR2[c]])
                dump('zb', l, ti, (lambda c: R2[:, c, 0:T]), bR2, T)
                for half in range(2):
                    w, s = wnext("g0%d" % half)
                    for mi in range(4):
                        m = half * 4 + mi
                        b = group(T, proj_pairs(w, mi, T), hn_reads, s)
                        act(GT[:, m, 0:T], PS[:, b, 0:T], AF.Sigmoid, [bPS[b]], [bGT[m]], bias=vcol(vb + O_BIN + B_G0 + m))
                for half in range(2):
                    w, s = wnext("woa%d" % half)
                    for mi in range(4):
                        m = half * 4 + mi
                        b = group(T, lin_pairs(w, mi, R2, T), list(bR2), s)
                        tt("dve", BIG[:, 8 + m, 0:T], PS[:, b, 0:T], GT[:, m, 0:T], ALU.mult, [bPS[b], bGT[m]], [bBIG[8 + m]])

                dump('mgA', l, ti, (lambda c: BIG[:, 8 + c, 0:T]), bBIG[8:16], T)
                prep_point()
                for half in range(2):
                    w, s = wnext("bg%d" % half)
                    for mi in range(4):
                        m = half * 4 + mi
                        b = group(T, proj_pairs(w, mi, T), hn_reads, s)
                        act(BIG[:, m, MG:MG + T], PS[:, b, 0:T], AF.Sigmoid, [bPS[b]], [bBIG[m]], bias=vcol(vb + O_BIN + B_BG + m))
                for half in range(2):
                    w, s = wnext("bv%d" % half)
                    for mi in range(4):
                        m = half * 4 + mi
                        b = group(T, proj_pairs(w, mi, T), hn_reads, s)
                        margin_in(R1[:, m, 0:MG], HB[:, m, :], bR1[m], bHB[m], ti)
                        stt("dve", R1[:, m, MG:MG + T], PS[:, b, 0:T], vcol(vb + O_BIN + B_BV + m), BIG[:, m, MG:MG + T],
                            ALU.add, ALU.mult, [bPS[b], bBIG[m]], [bR1[m]])
                        if ti == 0:
                            tt("pool", R1[:, m, MG:MG + HALO], R1[:, m, MG:MG + HALO], MASK[:, :], ALU.mult, [bR1[m]], [bR1[m]])
                        margin_out(HB[:, m, :], R1[:, m, T:T + MG], bHB[m], bR1[m])
                dump('u', l, ti, (lambda c: R1[:, c, MG:MG + T]), bR1, T)
                for c in range(NCH):
                    w, s = wnext("diag_b%d" % c)
                    pairs = [(w[:, k * 128:(k + 1) * 128], R1[:, c, MG - (KB - 1) + k: MG - (KB - 1) + k + T]) for k in range(ND, KB)]
                    b = group(T, pairs, [bR1[c]], s)
                    kacc = c % 2
                    for k in range(ND):
                        ush = R1[:, c, MG - (KB - 1) + k: MG - (KB - 1) + k + T]
                        wk = vcol(vb + O_CB + k * 8 + c)
                        if k == 0:
                            ts("dve", PP[:, kacc, 0:T], ush, wk, None, ALU.mult, None, [bR1[c]], [bPP[kacc]])
                        else:
                            stt("dve", PP[:, kacc, 0:T], ush, wk, PP[:, kacc, 0:T], ALU.mult, ALU.add, [bR1[c], bPP[kacc]], [bPP[kacc]])
                    stt("dve", BIG[:, c, MG:MG + T], PS[:, b, 0:T], vcol(vb + O_CBB + c), PP[:, kacc, 0:T], ALU.add, ALU.add,
                        [bPS[b], bPP[kacc]], [bBIG[c]])
                    act(GT[:, c, 0:T], BIG[:, c, MG:MG + T], AF.Square, [bBIG[c]], [bGT[c]])
                    cp("pool", R2[:, c, 0:T], BIG[:, c, MG:MG + T], [bBIG[c]], [bR2[c]])
                dump('v', l, ti, (lambda c: BIG[:, c, MG:MG + T]), bBIG[0:8], T)
                b1 = group(T, [(ONES[:, :], R2[:, c, 0:T]) for c in range(NCH)], list(bR2) + [bCONST])
                b2 = group(T, [(ONES[:, :], GT[:, c, 0:T]) for c in range(NCH)], list(bGT) + [bCONST])
                ts("dve", SCR[:, 2, 0:T], PS[:, b1, 0:T], 1.0 / D, None, ALU.mult, None, [bPS[b1]], [bSCR[2]])
                tt("dve", SCR[:, 4, 0:T], SCR[:, 2, 0:T], SCR[:, 2, 0:T], ALU.mult, [bSCR[2]], [bSCR[4]])
                stt("dve", SCR[:, 5, 0:T], PS[:, b2, 0:T], 1.0 / D, SCR[:, 4, 0:T], ALU.mult, ALU.subtract, [bPS[b2], bSCR[4]], [bSCR[5]])
                act(SCR[:, 0, 0:T], SCR[:, 5, 0:T], AF.Sqrt, [bSCR[5]], [bSCR[0]], bias=vcol(O_EPS), scale=1.0)
                P.add("dve", (lambda e, T=T: e.reciprocal(out=SCR[:, 3, 0:T], in_=SCR[:, 0, 0:T])), [bSCR[0]], [bSCR[3]])
                for c in range(NCH):
                    tt("pool", BIG[:, c, MG:MG + T], BIG[:, c, MG:MG + T], SCR[:, 2, 0:T], ALU.subtract, [bBIG[c], bSCR[2]], [bBIG[c]])
                    tt("dve", BIG[:, c, MG:MG + T], BIG[:, c, MG:MG + T], SCR[:, 3, 0:T], ALU.mult, [bBIG[c], bSCR[3]], [bBIG[c]])
                    act(R2[:, c, 0:T], BIG[:, c, MG:MG + T], AF.Silu, [bBIG[c]], [bR2[c]],
                        bias=vcol(vb + O_LNB + c), scale=vcol(vb + O_LNG + c))
                dump('sb', l, ti, (lambda c: R2[:, c, 0:T]), bR2, T)
                for half in range(2):
                    w, s = wnext("ci%d" % half)
                    for mi in range(4):
                        m = half * 4 + mi
                        b = group(T, proj_pairs(w, mi, T), hn_reads, s)
                        margin_in(BIG[:, m, 0:MG], HC[:, m, :], bBIG[m], bHC[m], ti)
                        act(BIG[:, m, MG:MG + T], PS[:, b, 0:T], AF.Identity, [bPS[b]], [bBIG[m]], bias=vcol(vb + O_BIN + B_CI + m))
                        if ti == 0:
                            tt("pool", BIG[:, m, MG:MG + HALO], BIG[:, m, MG:MG + HALO], MASK[:, :], ALU.mult, [bBIG[m]], [bBIG[m]])
                        margin_out(HC[:, m, :], BIG[:, m, T:T + MG], bHC[m], bBIG[m])
                for c in range(NCH):
                    g = c // 2
                    wdw = POOLW[g]
                    src_ap, src_b = BIG[:, c, :], bBIG[c]
                    lo = MG - 15
                    for jstep in range(g + 1):
                        sh = 1 << jstep
                        lo += sh
                        k = jstep % 2
                        tt("dve", PP[:, k, lo:MG + T], src_ap[:, lo:MG + T], src_ap[:, lo - sh:MG + T - sh], ALU.add,
                           [src_b], [bPP[k]])
                        src_ap, src_b = PP[:, k, :], bPP[k]
                    stt("dve", R1[:, c, 0:T], src_ap[:, MG:MG + T], 1.0 / wdw, BIG[:, c, MG:MG + T], ALU.mult, ALU.subtract,
                        [src_b, bBIG[c]], [bR1[c]])
                    if ti == 0:
                        tt("dve", src_ap[:, MG:MG + 128], src_ap[:, MG:MG + 128], ICNT[:, g, :], ALU.mult, [src_b], [src_b])
                        tt("dve", R1[:, c, 0:128], src_ap[:, MG:MG + 128], BIG[:, c, MG:MG + 128], ALU.subtract,
                           [src_b, bBIG[c]], [bR1[c]])
                dump('mgB', l, ti, (lambda c: BIG[:, 8 + c, 0:T]), bBIG[8:16], T)
                prep_point()
                dump('cin', l, ti, (lambda c: BIG[:, c, MG:MG + T]), bBIG[0:8], T)
                for half in range(2):
                    w, s = wnext("g2%d" % half)
                    for mi in range(4):
                        m = half * 4 + mi
                        b = group(T, proj_pairs(w, mi, T), hn_reads, s)
                        act(GT[:, m, 0:T], PS[:, b, 0:T], AF.Sigmoid, [bPS[b]], [bGT[m]], bias=vcol(vb + O_BIN + B_G2 + m))
                dump('pf', l, ti, (lambda c: R1[:, c, 0:T]), bR1, T)
                w, s = wnext("pool")
                for g in range(4):
                    for mo in range(2):
                        m = 2 * g + mo
                        pairs = [(w[:, (g * 2 + kc) * 256 + mo * 128:(g * 2 + kc) * 256 + (mo + 1) * 128], R1[:, 2 * g + kc, 0:T])
                                 for kc in range(2)]
                        b = group(T, pairs, [bR1[2 * g], bR1[2 * g + 1]], s)
                        k = 4 + (m % 2)
                        stt("dve", SCR[:, k, 0:T], PS[:, b, 0:T], vcol(vb + O_PS + m), GT[:, m, 0:T], ALU.mult, ALU.mult,
                            [bPS[b], bGT[m]], [bSCR[k]])
                        tt("pool", BIG[:, 8 + m, 0:T], BIG[:, 8 + m, 0:T], SCR[:, k, 0:T], ALU.add, [bBIG[8 + m], bSCR[k]], [bBIG[8 + m]])

                for half in range(2):
                    w, s = wnext("g1%d" % half)
                    for mi in range(4):
                        m = half * 4 + mi
                        b = group(T, proj_pairs(w, mi, T), hn_reads, s)
                        act(GT[:, m, 0:T], PS[:, b, 0:T], AF.Sigmoid, [bPS[b]], [bGT[m]], bias=vcol(vb + O_BIN + B_G1 + m))
                for half in range(2):
                    w, s = wnext("wob%d" % half)
                    for mi in range(4):
                        m = half * 4 + mi
                        b = group(T, lin_pairs(w, mi, R2, T), list(bR2), s)
                        k = 4 + (m % 2)
                        stt("dve", SCR[:, k, 0:T], PS[:, b, 0:T], vcol(vb + O_BOB + m), GT[:, m, 0:T], ALU.add, ALU.mult,
                            [bPS[b], bGT[m]], [bSCR[k]])
                        tt("dve", R1[:, m, 0:T], BIG[:, 8 + m, 0:T], SCR[:, k, 0:T], ALU.add, [bBIG[8 + m], bSCR[k]], [bR1[m]])

                dump('mgb', l, ti, (lambda c: R2[:, c, 0:T]), bR2, T)
                for half in range(2):
                    w, s = wnext("wo%d" % half)
                    for mi in range(4):
                        m = half * 4 + mi
                        b = group(T, lin_pairs(w, mi, R1, T), list(bR1), s)
                        tt("dve", X[:, m, off:off + T], X[:, m, off:off + T], PS[:, b, 0:T], ALU.add, [bPS[b], bX[ti][m]], [bX[ti][m]])
                        act(HN[:, m, 0:T], X[:, m, off:off + T], AF.Identity, [bX[ti][m]], [bHN[m]], scale=vcol(vb + O_GMLP + m))

                dump('x1', l, ti, (lambda c: X[:, c, off:off + T]), bX[ti], T)
                prep_point()
                if nxt is not None:
                    noff, nT = TILES[nxt[1]]
                for i in range(8):
                    if i == 1:
                        for c in range(NCH):
                            act(R2[:, c, 0:T], X[:, c, off:off + T], AF.Square, [bX[ti][c]], [bR2[c]])
                    if i == 5 and nxt is not None:
                        norm_sq(nxt[1], noff, nT)
                    if i == 3:
                        bss = group(T, [(ONES[:, :], R2[:, c, 0:T]) for c in range(NCH)], list(bR2) + [bCONST])
                        ts("dve", SCR[:, 2, 0:T], PS[:, bss, 0:T], 1.0 / D, EPS, ALU.mult, ALU.add, [bPS[bss]], [bSCR[2]])
                        P.add("dve", (lambda e, T=T: e.reciprocal(out=SCR[:, 3, 0:T], in_=SCR[:, 2, 0:T])), [bSCR[2]], [bSCR[3]])
                    w, s = wnext("m1_%d" % i)
                    for mi in range(4):
                        jh = i * 4 + mi
                        b = group(T, proj_pairs(w, mi, T), hn_reads, s)
                        act(hid_ap(jh, T), PS[:, b, 0:T], AF.Relu, [bPS[b]], [bBIG[jh // 2]])
                        tt("dve", hid_ap(jh, T), hid_ap(jh, T), hid_ap(jh, T), ALU.mult, [bBIG[jh // 2]], [bBIG[jh // 2]])
                if nxt is not None:
                    norm_rest(nT)
                for h in range(2):
                    banks = [next_bank() for _ in range(4)]
                    P.deps("pe", [], [bPS[b] for b in banks])
                    for kq in range(4):
                        P.deps("pe", bBIG[kq * 4:(kq + 1) * 4], [])
                        w, s = wnext("m2_%d_%d" % (h, kq))
                        for mi in range(4):
                            for kc in range(NCH):
                                last = (kq == 3 and kc == NCH - 1)
                                endu = (mi == 3 and kc == NCH - 1)
                                my = mm(banks[mi], T, w[:, kc * 512 + mi * 128: kc * 512 + (mi + 1) * 128],
                                        hid_ap(kq * 8 + kc, T), kq == 0 and kc == 0, last, inc=(last or endu))
                                if last:
                                    bPS[banks[mi]].w = ("pe", my)
                                    bPS[banks[mi]].r = {}
                        slot_last[s] = my
                    for bf in bBIG:
                        bf.r["pe"] = my
                    if h == 0 and nxt is not None:
                        norm_apply(nxt[1], noff, nT, nxt[0] * VPL + O_GMIX, hn_fn(nT), bHN)
                    for mi in range(4):
                        m = h * 4 + mi
                        b = banks[mi]
                        k = 4 + (m % 2)
                        tt("dve", SCR[:, k, 0:T], PS[:, b, 0:T], SCR[:, 3, 0:T], ALU.mult, [bPS[b], bSCR[3]], [bSCR[k]])
                        tt("dve", X[:, m, off:off + T], X[:, m, off:off + T], SCR[:, k, 0:T], ALU.add, [bSCR[k], bX[ti][m]], [bX[ti][m]])
                dump('x2', l, ti, (lambda c: X[:, c, off:off + T]), bX[ti], T)
                assert cur["ui"] == NU
                prep_point()

                if l == L - 1:
                    rmsnorm_to(ti, off, T, O_GFIN, (lambda c, T=T: BIG[:, 8 + c, MG:MG + T]), bBIG[8:16])
                    lo = HALO if ti == 0 else 0
                    o0 = off + lo - HALO
                    n = T - lo
                    P.deps("pool", [bBIG[8 + c] for c in range(NCH)], [])
                    key = ("out", ti)
                    v = P.emit("pool", (lambda e, lo=lo, o0=o0, n=n: e.dma_start(out=yT[:, :, o0:o0 + n],
                                                                                   in_=BIG[:, NCH:2 * NCH, MG + lo:MG + lo + n])),
                               inc=False, dmakey=key)
                    for c in range(NCH):
                        bBIG[8 + c].dmar = (key, v)
        assert not pending and not pending0 and not xpending and not conv_queue
        for ti in range(len(TILES)):
            P.wait("pool", ("out", ti), 16)

        keys = set()
        for e in Prog.ENGS:
            for it in P.ops[e]:
                if it[0] == "wait":
                    keys.add(it[1])
                elif it[3] is not None:
                    keys.add(it[3])
        for e in ("pe", "act", "dve", "pool"):
            keys.add(e)
        sems = {}
        for k in sorted(keys, key=str):
            nm = "s_" + str(k).replace("(", "").replace(")", "").replace(",", "_").replace("'", "").replace(" ", "")
            sems[k] = es.enter_context(nc.semaphore(nm))

        def replay(name, eng):
            for it in P.ops[name]:
                if it[0] == "wait":
                    eng.wait_ge(sems[it[1]], it[2])
                else:
                    ins = it[1](eng)
                    if it[3] is not None:
                        ins.then_inc(sems[it[3]], 16)
                    elif it[2]:
                        ins.then_inc(sems[name], 1)

        with nc.Block() as block:
            @block.tensor
            def _(e):
                replay("pe", e)

            @block.scalar
            def _(e):
                replay("act", e)

            @block.vector
            def _(e):
                replay("dve", e)

            @block.gpsimd
            def _(e):
                replay("pool", e)

            @block.sync
            def _(e):
                replay("sp", e)
    return nc


def _chunkvec(v):
    return np.ascontiguousarray(v.reshape(-1, 128).T)


def _unit_from(wmat, colstart):
    return wmat.reshape(NCH, 128, -1)[:, :, colstart:colstart + 512].transpose(1, 0, 2).reshape(128, UC)


def prepare_inputs(x, g_mix, w_in, b_in, conv_a, w_out_a, conv_b, conv_b_bias, ln_b_g, ln_b_b,
                   w_out_b, b_out_b, w_pool, pool_scale, w_o, g_mlp, w_mlp1, w_mlp2, g_final):
    f = np.float32
    wsrc = np.zeros((L, NSRC, 128, UC), dtype=f)
    for l in range(L):
        mats = {"win": w_in[l], "woa": w_out_a[l], "wob": w_out_b[l], "wo": w_o[l], "mlp1": w_mlp1[l]}
        for ui, (name, ncols, src) in enumerate(UNITS):
            if src is None:
                continue
            si = SRC_IDX[ui]
            kind, arg = src
            if kind in mats:
                wsrc[l, si] = _unit_from(np.asarray(mats[kind], dtype=f), arg)
            elif kind == "mlp2":
                h, kq = arg
                wsrc[l, si] = np.asarray(w_mlp2[l], dtype=f).reshape(32, 128, D)[kq * 8:(kq + 1) * 8, :, h * 512:(h + 1) * 512] \
                    .transpose(1, 0, 2).reshape(128, UC)
            elif kind == "pool":
                wsrc[l, si, :, 0:2048] = np.asarray(w_pool[l], dtype=f).reshape(4, 2, 128, 256).transpose(2, 0, 1, 3).reshape(128, 2048)
    vecs = np.zeros((128, NV), dtype=f)
    for l in range(L):
        vb = l * VPL
        vecs[:, vb + O_GMIX:vb + O_GMIX + 8] = _chunkvec(g_mix[l])
        vecs[:, vb + O_BIN:vb + O_BIN + 72] = _chunkvec(b_in[l])
        for k in range(KA):
            vecs[:, vb + O_CA + k * 8:vb + O_CA + (k + 1) * 8] = _chunkvec(conv_a[l, k])
        for k in range(KB):
            vecs[:, vb + O_CB + k * 8:vb + O_CB + (k + 1) * 8] = _chunkvec(conv_b[l, k])
        vecs[:, vb + O_CBB:vb + O_CBB + 8] = _chunkvec(conv_b_bias[l])
        vecs[:, vb + O_LNG:vb + O_LNG + 8] = _chunkvec(ln_b_g[l])
        vecs[:, vb + O_LNB:vb + O_LNB + 8] = _chunkvec(ln_b_b[l])
        vecs[:, vb + O_BOB:vb + O_BOB + 8] = _chunkvec(b_out_b[l])
        vecs[:, vb + O_PS:vb + O_PS + 8] = _chunkvec(pool_scale[l])
        vecs[:, vb + O_GMLP:vb + O_GMLP + 8] = _chunkvec(g_mlp[l])
    vecs[:, O_GFIN:O_GFIN + 8] = _chunkvec(g_final)
    vecs[:, O_EPS:O_EPS + 4] = EPS
    ident = np.eye(128, dtype=f)
    in_maps = []
    for core in range(NCORE):
        bi, ch = core // 4, core % 4
        s0 = ch * CHUNK_TOK
        lo = s0 - HALO
        xt = np.zeros((NTOK, D), dtype=f)
        if lo >= 0:
            xt[:] = x[bi, lo:lo + NTOK]
        else:
            xt[-lo:] = x[bi, 0:NTOK + lo]
        xTl = np.ascontiguousarray(xt.T.reshape(NCH, 128, NTOK).transpose(1, 0, 2))
        pos = lo + np.arange(128)
        mask = np.broadcast_to((pos[:HALO] >= 0).astype(f)[None, :], (128, HALO)).copy()
        icnt = np.zeros((128, 4, 128), dtype=f)
        for g, wdw in enumerate(POOLW):
            cnt = np.where(pos >= 0, np.minimum(pos + 1, wdw), wdw).astype(f)
            icnt[:, g, :] = (1.0 / cnt)[None, :]
        in_maps.append({"xT": xTl, "wsrc": wsrc, "vecs": vecs, "ident": ident, "mask": mask, "icnt": icnt})
    return in_maps


_NC_CACHE = {}


def kernel(**inputs):
    inputs = {k: np.asarray(v) for k, v in inputs.items()}
    in_maps = prepare_inputs(**inputs)
    if "nc" not in _NC_CACHE:
        _NC_CACHE["nc"] = build_program()
    nc = _NC_CACHE["nc"]
    res = run_bass_kernel_spmd(nc, in_maps, core_ids=list(range(NCORE)))
    out = np.empty((2, SEQ, D), dtype=np.float32)
    for core in range(NCORE):
        bi, ch = core // 4, core % 4
        yT = np.asarray(res.results[core]["yT"])
        out[bi, ch * CHUNK_TOK:(ch + 1) * CHUNK_TOK, :] = yT.transpose(2, 1, 0).reshape(CHUNK_TOK, D)
    return out
```
